# Optimizing a Trainium2 kernel written in Bass

```python
import math
import jax, jax.numpy as jnp
from jax import lax
import numpy as np

D_MODEL = 1024
BATCH = 2
SEQ = 8192
DEPTH = 2

MEM_LEN = 256
HG_HEADS = 8
HG_KDIM = 128
HG_VDIM = 128
HG_QK_W = HG_HEADS * HG_KDIM
HG_WIDTH = HG_HEADS * HG_VDIM
HG_CHUNK = 64
MLA_HEADS = 8
MLA_NOPE = 64
MLA_ROPE = 32
MLA_VDIM = 64
MLA_WIDTH = MLA_HEADS * MLA_VDIM
MLA_Q_RANK = 256
MLA_KV_RANK = 128
ROPE_THETA = 10000.0
Q_BLOCK = 128
MEM_HEADS = 4
MEM_HDIM = 128
MEM_WIDTH = MEM_HEADS * MEM_HDIM
MIX_WIDTH = HG_WIDTH + MLA_WIDTH + MEM_WIDTH
IN_SIZES = (HG_QK_W, HG_QK_W, HG_WIDTH, HG_WIDTH,
            MLA_Q_RANK, MLA_KV_RANK, MLA_ROPE, MLA_WIDTH,
            MEM_WIDTH, MEM_WIDTH)
IN_WIDTH = sum(IN_SIZES)
IN_OFFSETS = tuple(int(v) for v in np.cumsum(IN_SIZES)[:-1])
DEEPNORM_ALPHA = (2 * DEPTH) ** 0.25
DEEPNORM_BETA = (8 * DEPTH) ** -0.25
RMS_EPS = 1e-6
LN_EPS = 1e-5
NEG_BIG = -1e30

kernel_name = 'hymba_hgrn2_mla_mem_deepnorm'


def _rmsnorm(t, g):
    t32 = t.astype(jnp.float32)
    t32 = t32 * lax.rsqrt(jnp.mean(t32 * t32, axis=-1, keepdims=True) + RMS_EPS)
    return t32.astype(t.dtype) * g


def _layernorm(t, g, b):
    t32 = t.astype(jnp.float32)
    mu = jnp.mean(t32, axis=-1, keepdims=True)
    var = jnp.mean(jnp.square(t32 - mu), axis=-1, keepdims=True)
    return ((t32 - mu) * lax.rsqrt(var + LN_EPS)).astype(t.dtype) * g + b


def _rope(t, cos, sin):
    half = t.shape[-1] // 2
    t1, t2 = t[..., :half], t[..., half:]
    return jnp.concatenate([t1 * cos - t2 * sin, t1 * sin + t2 * cos], axis=-1)


def _chunk_gated_recurrence(q, k, v, logf):
    B, S, H, dk = q.shape
    dv = v.shape[-1]
    C = HG_CHUNK
    n = S // C

    def to_chunks(t):
        return t.reshape(B, n, C, H, t.shape[-1]).transpose(1, 0, 3, 2, 4)

    tri = jnp.tril(jnp.ones((C, C), dtype=jnp.float32))

    def step(state, inp):
        qc, kc, vc, gc = inp
        b = jnp.cumsum(gc, axis=2)
        diff = b[:, :, :, None, :] - b[:, :, None, :, :]
        decay = jnp.exp(jnp.minimum(diff, 0.0)) * tri[:, :, None]
        A = jnp.einsum('bhtsd,bhsd->bhts', qc[:, :, :, None, :] * decay, kc)
        o = (jnp.einsum('bhts,bhsv->bhtv', A, vc)
             + jnp.einsum('bhtd,bhdv->bhtv', qc * jnp.exp(b), state))
        b_last = b[:, :, -1:, :]
        state = (jnp.exp(b_last[:, :, 0, :])[..., None] * state
                 + jnp.einsum('bhsd,bhsv->bhdv', kc * jnp.exp(b_last - b), vc))
        return state, o

    state0 = jnp.zeros((B, H, dk, dv), jnp.float32)
    _, o = lax.scan(step, state0, (to_chunks(q), to_chunks(k), to_chunks(v), to_chunks(logf)))
    return o.transpose(1, 0, 3, 2, 4).reshape(B, S, H, dv)


def _hgrn2_group(q, f_pre, i, lb, out_g):
    B, S, _ = q.shape
    q32 = q.reshape(B, S, HG_HEADS, HG_KDIM).astype(jnp.float32)
    z = f_pre.reshape(B, S, HG_HEADS, HG_KDIM).astype(jnp.float32)
    lb = lb.reshape(HG_HEADS, HG_KDIM)
    logf = jax.nn.log_sigmoid(z) + jnp.log1p(lb * jnp.exp(-z))
    k = (1.0 - lb) * jax.nn.sigmoid(-z)
    v = i.reshape(B, S, HG_HEADS, HG_VDIM).astype(jnp.float32)
    o = _chunk_gated_recurrence(q32, k, v, logf)
    o = o * lax.rsqrt(jnp.mean(o * o, axis=-1, keepdims=True) + RMS_EPS)
    return (o.astype(q.dtype) * out_g).reshape(B, S, HG_WIDTH)


def _mla_group(cq, ckv, kr, q_norm, kv_norm, w_uq, w_uk, w_uv, cos, sin):
    B, S, _ = cq.shape
    q = (_rmsnorm(cq, q_norm) @ w_uq).reshape(B, S, MLA_HEADS, MLA_NOPE + MLA_ROPE)
    q_nope = q[..., :MLA_NOPE]
    q_rope = _rope(q[..., MLA_NOPE:], cos[:, :, None, :], sin[:, :, None, :])
    c = _rmsnorm(ckv, kv_norm)
    k_nope = (c @ w_uk).reshape(B, S, MLA_HEADS, MLA_NOPE)
    v = (c @ w_uv).reshape(B, S, MLA_HEADS, MLA_VDIM)
    k_rope = _rope(kr, cos, sin)
    scale = 1.0 / math.sqrt(MLA_NOPE + MLA_ROPE)
    nb = S // Q_BLOCK
    qn_b = q_nope.reshape(B, nb, Q_BLOCK, MLA_HEADS, MLA_NOPE).transpose(1, 0, 2, 3, 4)
    qr_b = q_rope.reshape(B, nb, Q_BLOCK, MLA_HEADS, MLA_ROPE).transpose(1, 0, 2, 3, 4)
    key_idx = jnp.arange(S)

    def attend(args):
        qn, qr, blk = args
        s = (jnp.einsum('bqhd,bkhd->bhqk', qn, k_nope)
             + jnp.einsum('bqhr,bkr->bhqk', qr, k_rope)).astype(jnp.float32) * scale
        q_idx = blk * Q_BLOCK + jnp.arange(Q_BLOCK)
        mask = key_idx[None, :] <= q_idx[:, None]
        s = jnp.where(mask, s, NEG_BIG)
        p = jax.nn.softmax(s, axis=-1).astype(v.dtype)
        return jnp.einsum('bhqk,bkhd->bqhd', p, v)

    o = lax.map(attend, (qn_b, qr_b, jnp.arange(nb)))
    return o.transpose(1, 0, 2, 3, 4).reshape(B, S, MLA_WIDTH)


def _memory_group(qm, mem, w_k, w_v):
    B, S, _ = qm.shape
    M = mem.shape[1]
    q = qm.reshape(B, S, MEM_HEADS, MEM_HDIM)
    k = (mem @ w_k).reshape(B, M, MEM_HEADS, MEM_HDIM)
    v = (mem @ w_v).reshape(B, M, MEM_HEADS, MEM_HDIM)
    s = jnp.einsum('bshd,bmhd->bhsm', q, k).astype(jnp.float32) / math.sqrt(MEM_HDIM)
    p = jax.nn.softmax(s, axis=-1).astype(v.dtype)
    return jnp.einsum('bhsm,bmhd->bshd', p, v).reshape(B, S, MEM_WIDTH)


def setup_inputs(seed: int = 0) -> dict:
    key = jax.random.key(seed)
    ks = jax.random.split(key, 16)
    nrm = jax.random.normal
    x = nrm(ks[0], (BATCH, SEQ, D_MODEL), jnp.float32)
    mem = nrm(ks[1], (BATCH, MEM_LEN, D_MODEL), jnp.float32)
    offsets = jax.random.randint(ks[2], (BATCH, 1), 0, 1024, dtype=jnp.int32)
    positions = (jnp.arange(SEQ, dtype=jnp.int32)[None, :] + offsets).astype(jnp.int32)
    w_in = nrm(ks[3], (DEPTH, D_MODEL, IN_WIDTH)) * D_MODEL ** -0.5
    hgrn_lb_logits = 0.1 * nrm(ks[4], (DEPTH, HG_QK_W))
    hgrn_out_norm = 1.0 + 0.02 * nrm(ks[5], (DEPTH, HG_VDIM))
    mla_q_norm = 1.0 + 0.02 * nrm(ks[6], (DEPTH, MLA_Q_RANK))
    mla_kv_norm = 1.0 + 0.02 * nrm(ks[7], (DEPTH, MLA_KV_RANK))
    w_mla_uq = nrm(ks[8], (DEPTH, MLA_Q_RANK, MLA_HEADS * (MLA_NOPE + MLA_ROPE))) * MLA_Q_RANK ** -0.5
    w_mla_uk = nrm(ks[9], (DEPTH, MLA_KV_RANK, MLA_HEADS * MLA_NOPE)) * MLA_KV_RANK ** -0.5
    w_mla_uv = nrm(ks[10], (DEPTH, MLA_KV_RANK, MLA_WIDTH)) * (MLA_KV_RANK ** -0.5 * DEEPNORM_BETA)
    w_mem_k = nrm(ks[11], (DEPTH, D_MODEL, MEM_WIDTH)) * D_MODEL ** -0.5
    w_mem_v = nrm(ks[12], (DEPTH, D_MODEL, MEM_WIDTH)) * (D_MODEL ** -0.5 * DEEPNORM_BETA)
    w_out = nrm(ks[13], (DEPTH, MIX_WIDTH, D_MODEL)) * (MIX_WIDTH ** -0.5 * DEEPNORM_BETA)
    ln_g = 1.0 + 0.02 * nrm(ks[14], (DEPTH, D_MODEL))
    ln_b = 0.02 * nrm(ks[15], (DEPTH, D_MODEL))
    return {'x': x, 'mem': mem, 'positions': positions, 'w_in': w_in,
            'hgrn_lb_logits': hgrn_lb_logits, 'hgrn_out_norm': hgrn_out_norm,
            'mla_q_norm': mla_q_norm, 'mla_kv_norm': mla_kv_norm,
            'w_mla_uq': w_mla_uq, 'w_mla_uk': w_mla_uk, 'w_mla_uv': w_mla_uv,
            'w_mem_k': w_mem_k, 'w_mem_v': w_mem_v, 'w_out': w_out,
            'ln_g': ln_g, 'ln_b': ln_b}


def reference(x, mem, positions, w_in, hgrn_lb_logits, hgrn_out_norm, mla_q_norm,
              mla_kv_norm, w_mla_uq, w_mla_uk, w_mla_uv, w_mem_k, w_mem_v, w_out,
              ln_g, ln_b):
    inv_freq = 1.0 / (ROPE_THETA ** (jnp.arange(0, MLA_ROPE, 2, dtype=jnp.float32) / MLA_ROPE))
    ang = positions.astype(jnp.float32)[..., None] * inv_freq
    cos = jnp.cos(ang).astype(x.dtype)
    sin = jnp.sin(ang).astype(x.dtype)
    lb_soft = jax.nn.softmax(hgrn_lb_logits.astype(jnp.float32), axis=0)
    lower_bounds = jnp.cumsum(lb_soft, axis=0) - lb_soft[0]

    for l in range(DEPTH):
        h = x @ w_in[l]
        (hg_q, hg_f, hg_i, hg_gate, mla_cq, mla_ckv, mla_kr, mla_gate,
         mem_q, mem_gate) = jnp.split(h, IN_OFFSETS, axis=-1)
        y_hg = _hgrn2_group(hg_q, hg_f, hg_i, lower_bounds[l], hgrn_out_norm[l])
        y_mla = _mla_group(mla_cq, mla_ckv, mla_kr, mla_q_norm[l], mla_kv_norm[l],
                           w_mla_uq[l], w_mla_uk[l], w_mla_uv[l], cos, sin)
        y_mem = _memory_group(mem_q, mem, w_mem_k[l], w_mem_v[l])
        y = jnp.concatenate([y_hg * jax.nn.silu(hg_gate),
                             y_mla * jax.nn.silu(mla_gate),
                             y_mem * jax.nn.silu(mem_gate)], axis=-1)
        y = y @ w_out[l]
        x = _layernorm(DEEPNORM_ALPHA * x + y, ln_g[l], ln_b[l])
    return x
```

```python
import math
from contextlib import ExitStack
import numpy as np
import concourse.bass as bass
import concourse.mybir as mybir
from concourse.bass_utils import run_bass_kernel_spmd

F32 = mybir.dt.float32
BF16 = mybir.dt.bfloat16
I32 = mybir.dt.int32
AF = mybir.ActivationFunctionType
ALU = mybir.AluOpType

S = 8192
D = 1024
TS = 512
NFM = 13
NCOL = NFM * 128 + 256
DEPTH = 2
ALPHA = (2 * DEPTH) ** 0.25
RMS_EPS = 1e-6
LN_EPS = 1e-5
SC_MLA = 1.0 / math.sqrt(96.0)
SC_MEM = 1.0 / math.sqrt(128.0)
TWO_PI = 2.0 * math.pi
CW1 = 6.28125
CW2 = TWO_PI - CW1
import os
STOP = int(os.environ.get("MK_STOP", "0"))


class Tk:
    __slots__ = ("name", "w", "r", "x")

    def __init__(self, name, x=False):
        self.name = name
        self.w = None
        self.r = []
        self.x = x


class Prog:
    ENGS = ("pe", "act", "dve", "pool", "sp")

    def __init__(self, nc, es):
        self.nc = nc
        self.es = es
        self.ops = {e: [] for e in self.ENGS}
        self.sems = {}
        self.cnt = {}
        self.known = {e: {} for e in self.ENGS}
        for e in self.ENGS:
            self._sem("E_" + e)

    def _sem(self, key):
        if key not in self.sems:
            self.sems[key] = self.es.enter_context(self.nc.semaphore(key))
            self.cnt[key] = 0
        return self.sems[key]

    def sb(self, name, shape, dt):
        return self.es.enter_context(self.nc.sbuf_tensor(name, list(shape), dt))

    def ps(self, name, shape, dt=F32):
        return self.es.enter_context(self.nc.psum_tensor(name, list(shape), dt))

    def _deps(self, eng, reads, writes, strict=False):
        deps = {}

        def add(ev, raw):
            if ev is None:
                return
            key, val, src = ev
            if src == eng and not raw and not strict:
                return
            if deps.get(key, 0) < val:
                deps[key] = val

        for t in reads:
            add(t.w, True)
            if t.x:
                for ev in t.r:
                    if ev[2] != eng:
                        add(ev, False)
        for t in writes:
            add(t.w, False)
            for ev in t.r:
                add(ev, False)
        out = []
        kn = self.known[eng]
        for key, val in deps.items():
            if kn.get(key, 0) >= val:
                continue
            kn[key] = val
            out.append((key, val))
        return out

    def _commit(self, ev, reads, writes):
        for t in reads:
            t.r.append(ev)
            if len(t.r) > 16:
                best = {}
                for k, v, s in t.r:
                    if k not in best or best[k][1] < v:
                        best[k] = (k, v, s)
                t.r = list(best.values())
        for t in writes:
            t.w = ev
            t.r = []

    def op(self, eng, fn, reads=(), writes=()):
        waits = self._deps(eng, reads, writes)
        key = "E_" + eng
        self.cnt[key] += 1
        ev = (key, self.cnt[key], eng)
        self.ops[eng].append((waits, fn, (key, 1)))
        self._commit(ev, reads, writes)
        return ev

    def dma(self, q, out, in_, reads=(), writes=(), key=None):
        key = "D_" + key
        self._sem(key)
        waits = self._deps(q, reads, writes, strict=True)
        if self.cnt[key] > 0 and self.known[q].get(key, 0) < self.cnt[key]:
            self.known[q][key] = self.cnt[key]
            waits.append((key, self.cnt[key]))
        self.cnt[key] += 16
        ev = (key, self.cnt[key], "dma")
        self.ops[q].append((waits, I("dma_start", out=out, in_=in_), (key, 16)))
        self._commit(ev, reads, writes)
        return ev

    def wait_all(self, eng, tks):
        waits = self._deps(eng, tks, (), strict=True)
        self.ops[eng].append((waits, None, None))

    def emit(self):
        nc = self.nc
        sems = self.sems
        ops = self.ops

        def run(engine, lst):
            for waits, fn, inc in lst:
                for key, val in waits:
                    engine.wait_ge(sems[key], val)
                if fn is None:
                    continue
                ins = fn(engine)
                ins.then_inc(sems[inc[0]], inc[1])

        with nc.Block() as block:
            @block.tensor
            def _(e):
                run(e, ops["pe"])

            @block.scalar
            def _(e):
                run(e, ops["act"])

            @block.vector
            def _(e):
                run(e, ops["dve"])

            @block.gpsimd
            def _(e):
                run(e, ops["pool"])

            @block.sync
            def _(e):
                run(e, ops["sp"])


def I(name, *a, **k):
    return lambda e: getattr(e, name)(*a, **k)


class Ring:
    def __init__(self, P, name, n, shape, dt, psum=False):
        self.bufs = []
        for i in range(n):
            t = P.ps(f"{name}{i}", shape, dt) if psum else P.sb(f"{name}{i}", shape, dt)
            self.bufs.append((t, Tk(f"{name}{i}", x=psum)))
        self.i = 0

    def next(self):
        b = self.bufs[self.i % len(self.bufs)]
        self.i += 1
        return b


def build_mixer(layer, nt):
    nc = bass.Bass("TRN2", target_bir_lowering=False)

    def dr(n, s, dt=F32, kind="ExternalInput"):
        return nc.dram_tensor(n, list(s), dt, kind=kind).ap()

    xT = dr("xT", [D, S])
    win = dr("win", [D, NCOL])
    lbl = dr("lbl", [128, 4])
    vecs = dr("vecs", [128, 8])
    wuq = dr("wuq", [256, 384])
    wukv = dr("wukv", [128, 384])
    memT = dr("memT", [D, 256])
    wmkv = dr("wmkv", [D, 256])
    posr = dr("posr", [32, S], I32)
    cmask = dr("cmask", [128, 512 + 128 + 896])
    yT = dr("yT", [512, S], kind="ExternalOutput")

    with ExitStack() as es:
        P = Prog(nc, es)
        op = P.op

        def sbt(name, shape, dt=F32):
            return P.sb(name, shape, dt), Tk(name)

        wb, t_wb = sbt("wb", [128, 8, NCOL], BF16)
        wst = Ring(P, "wst", 2, [128, 8, 128], F32)
        lbt, t_lbt = sbt("lbt", [128, 4])
        lb, t_lb = sbt("lb", [128, 2])
        oml, t_oml = sbt("oml", [128, 2])
        vc, t_vc = sbt("vc", [128, 8])
        cm32, t_cm32 = sbt("cm32", [128, 640])
        caus, t_caus = sbt("caus", [128, 896], BF16)
        ident, t_ident = sbt("ident", [128, 128], BF16)
        ones_b, t_ones = sbt("ones_b", [128, 128], BF16)
        ones_f, t_onesf = sbt("ones_f", [128, 128], F32)
        pi_t, t_pi = sbt("pi_t", [128, 1])
        wuqb, t_wuqb = sbt("wuqb", [128, 2, 384], BF16)
        wukvb, t_wukvb = sbt("wukvb", [128, 384], BF16)
        memb, t_memb = sbt("memb", [128, 8, 256], BF16)
        wmkvb, t_wmkvb = sbt("wmkvb", [128, 8, 256], BF16)
        kmT, t_kmT = sbt("kmT", [128, 256], BF16)
        vm, t_vm = sbt("vm", [128, 2, 128], BF16)

        scanm = cm32[:, 0:512]
        mask2 = cm32[:, 512:640]

        P.dma("sp", lbt[:], lbl, writes=[t_lbt], key="c0")
        P.dma("sp", vc[:], vecs, writes=[t_vc], key="c1")
        P.dma("sp", cm32[:], cmask[:, 0:640], writes=[t_cm32], key="c2")
        st, t_st = wst.next()
        stf = st[:].rearrange("p a b -> p (a b)")
        P.dma("sp", stf[:, 0:896], cmask[:, 640:1536], writes=[t_st], key="wst0")
        op("pool", I("tensor_copy", out=caus[:], in_=stf[:, 0:896]), [t_st], [t_caus])
        op("dve", I("tensor_tensor", out=ident[:], in0=stf[:, 384:512], in1=stf[:, 383:511], op=ALU.subtract),
           [t_st], [t_ident])
        st2, t_st2 = wst.next()
        stf2 = st2[:].rearrange("p a b -> p (a b)")
        P.dma("sp", stf2[:, 0:768].rearrange("p (c n) -> p c n", c=2), wuq.rearrange("(c p) n -> p c n", p=128), writes=[t_st2], key="wst1")
        op("pool", I("tensor_copy", out=wuqb[:], in_=stf2[:, 0:768].rearrange("p (c n) -> p c n", c=2)), [t_st2], [t_wuqb])
        st3, t_st3 = wst.next()
        stf3 = st3[:].rearrange("p a b -> p (a b)")
        P.dma("sp", stf3[:, 0:384], wukv, writes=[t_st3], key="wst0")
        op("pool", I("tensor_copy", out=wukvb[:], in_=stf3[:, 0:384]), [t_st3], [t_wukvb])
        op("pool", I("memset", ones_b[:], 1.0), [], [t_ones])
        op("pool", I("memset", ones_f[:], 0.0), [], [t_onesf])
        op("pool", I("memset", ones_f[64:65, :], 1.0), [t_onesf], [t_onesf])
        op("pool", I("memset", pi_t[:], math.pi), [], [t_pi])
        if layer == 0:
            op("dve", I("memset", lb[:], 0.0), [], [t_lb])
            op("dve", I("memset", oml[:], 1.0), [], [t_oml])
        else:
            mx, t_mx = sbt("lbmx", [128, 2])
            ee, t_ee = sbt("lbee", [128, 4])
            op("dve", I("tensor_tensor", out=mx[:], in0=lbt[:, 0:2], in1=lbt[:, 2:4], op=ALU.max), [t_lbt], [t_mx])
            op("dve", I("tensor_tensor", out=ee[:, 0:2], in0=lbt[:, 0:2], in1=mx[:], op=ALU.subtract), [t_lbt, t_mx], [t_ee])
            op("dve", I("tensor_tensor", out=ee[:, 2:4], in0=lbt[:, 2:4], in1=mx[:], op=ALU.subtract), [t_lbt, t_mx, t_ee], [t_ee])
            op("act", I("activation", out=ee[:], in_=ee[:], func=AF.Exp), [t_ee], [t_ee])
            op("dve", I("tensor_tensor", out=mx[:], in0=ee[:, 0:2], in1=ee[:, 2:4], op=ALU.add), [t_ee], [t_mx])
            op("dve", I("reciprocal", out=mx[:], in_=mx[:]), [t_mx], [t_mx])
            op("dve", I("tensor_tensor", out=lb[:], in0=ee[:, 2:4], in1=mx[:], op=ALU.mult), [t_ee, t_mx], [t_lb])
            op("dve", I("tensor_scalar", out=oml[:], in0=lb[:], scalar1=-1.0, scalar2=1.0, op0=ALU.mult, op1=ALU.add), [t_lb], [t_oml])

        for c in range(NCOL // 128):
            st, t_st = wst.next()
            P.dma("sp", st[:], win[:, c * 128:(c + 1) * 128].rearrange("(c p) n -> p c n", p=128),
                  writes=[t_st], key=f"wst{c % 2}")
            op("dve" if c % 2 == 0 else "pool",
               I("tensor_copy", out=wb[:, :, c * 128:(c + 1) * 128], in_=st[:]),
               [t_st], [t_wb])
        for src, dst, t_dst in ((memT, memb, t_memb), (wmkv, wmkvb, t_wmkvb)):
            for c in range(2):
                st, t_st = wst.next()
                P.dma("sp", st[:], src[:, c * 128:(c + 1) * 128].rearrange("(c p) n -> p c n", p=128),
                      writes=[t_st], key=f"wst{wst.i % 2}")
                op("dve", I("tensor_copy", out=dst[:, :, c * 128:(c + 1) * 128], in_=st[:]),
                   [t_st], [t_dst])

        gen = Ring(P, "pgen", 3, [128, 512], F32, psum=True)
        psc = Ring(P, "psc", 2, [128, 512], F32, psum=True)
        pho, t_pho = P.ps("pho", [128, 512], F32), Tk("pho", x=True)
        po = [(P.ps(f"po{h}", [128, 512], F32), Tk(f"po{h}", x=True)) for h in range(2)]

        pk, t_pk = gen.next()
        for kc in range(8):
            op("pe", I("matmul", pk[:, 0:256], lhsT=wmkvb[:, kc, 0:128], rhs=memb[:, kc, :],
                                                start=(kc == 0), stop=(kc == 7)), [t_wmkvb, t_memb], [t_pk])
        op("act", I("activation", out=kmT[:], in_=pk[:, 0:256], func=AF.Copy), [t_pk], [t_kmT])
        pv, t_pv = gen.next()
        for mt in range(2):
            for kc in range(8):
                op("pe", I("matmul", pv[:, mt * 128:(mt + 1) * 128], lhsT=memb[:, kc, mt * 128:(mt + 1) * 128],
                                                           rhs=wmkvb[:, kc, 128:256], start=(kc == 0), stop=(kc == 7)),
                   [t_wmkvb, t_memb], [t_pv])
        op("act", I("activation", out=vm[:], in_=pv[:, 0:256].rearrange("p (m d) -> p m d", m=2), func=AF.Copy), [t_pv], [t_vm])

        KT = [sbt(f"KT{h}", [96, S], BF16) for h in range(2)]
        VV, t_VV = sbt("VV", [128, S // 128, 2, 65], BF16)
        op("pool", I("memset", VV[:], 1.0), [], [t_VV])
        St = [sbt(f"St{h}", [128, 128]) for h in range(2)]
        Sb = [[sbt(f"Sb{h}_{i}", [128, 128], BF16) for i in range(2)] for h in range(2)]
        sb_i = [0, 0]
        for h in range(2):
            op("dve", I("memset", St[h][0][:], 0.0), [], [St[h][1]])
            op("pool", I("memset", Sb[h][0][0][:], 0.0), [], [Sb[h][0][1]])

        x32 = Ring(P, "x32", 2, [128, 2, TS], F32)
        xbt, t_xb = sbt("xb", [128, 8, TS], BF16)
        W = {}
        for nm, n, dt in (("tmp", 8, F32), ("q32", 2, F32), ("zf", 2, F32), ("k32", 2, F32), ("bb", 2, F32),
                          ("eb", 2, F32), ("sg", 5, F32), ("yo", 2, F32), ("rope", 2, F32),
                          ("qt", 2, BF16), ("kt", 2, BF16), ("kh", 2, BF16), ("sq", 2, BF16), ("cn", 3, BF16),
                          ("qmb", 1, BF16), ("QT", 2, BF16), ("pt", 3, BF16)):
            W[nm] = Ring(P, nm, n, [128, TS], dt)
        am_r = Ring(P, "am", 2, [128, 128], BF16)
        vtm = Ring(P, "vtm", 2, [128, 4, 256], BF16)
        ktmA = Ring(P, "ktmA", 2, [128, 128], BF16)
        ktmB = Ring(P, "ktmB", 2, [128, 128], BF16)
        for (tt, tk_) in ktmA.bufs + ktmB.bufs:
            op("pool", I("memset", tt[:], 0.0), [], [tk_])
        posi, t_posi = sbt("posi", [128, TS], I32)
        y_tks = []
        eps_t, t_eps = sbt("eps_t", [128, 1])
        op("pool", I("memset", eps_t[:], RMS_EPS), [], [t_eps])
        tmp = W["tmp"]

        def silu_from(pc, t_pc, rows):
            ge, t_ge = tmp.next()
            g32, t_g32 = tmp.next()
            sg, t_sg = W["sg"].next()
            op("act", I("activation", out=ge[0:rows, :], in_=pc[0:rows, :], func=AF.Exp, scale=-1.0), [t_pc], [t_ge])
            op("act", I("activation", out=g32[0:rows, :], in_=pc[0:rows, :], func=AF.Copy), [t_pc], [t_g32])
            op("dve", I("tensor_scalar", out=ge[0:rows, :], in0=ge[0:rows, :], scalar1=1.0, scalar2=None, op0=ALU.add), [t_ge], [t_ge])
            op("dve", I("reciprocal", out=ge[0:rows, :], in_=ge[0:rows, :]), [t_ge], [t_ge])
            op("pool", I("tensor_tensor", out=sg[0:rows, :], in0=g32[0:rows, :], in1=ge[0:rows, :], op=ALU.mult), [t_g32, t_ge], [t_sg])
            return sg, t_sg

        def rstd_from(ms, t_ms, n):
            rs, t_rs = tmp.next()
            op("act", I("activation", out=rs[:], in_=ms[:], func=AF.Ln, bias=eps_t[:], scale=1.0 / n), [t_ms, t_eps], [t_rs])
            op("act", I("activation", out=rs[:], in_=rs[:], func=AF.Exp, scale=-0.5), [t_rs], [t_rs])
            return rs, t_rs

        R = slice(64, 96)
        n_q = 4

        def load_x(ti):
            tsl_ = slice(ti * TS, (ti + 1) * TS)
            for qd in range(n_q):
                xs, t_xs = x32.next()
                P.dma("sp", xs[:], xT[qd * 256:(qd + 1) * 256, tsl_].rearrange("(c p) t -> p c t", p=128),
                      writes=[t_xs], key=f"x{qd % 2}")
                op("pool" if qd % 2 else "dve", I("tensor_copy", out=xbt[:, qd * 2:(qd + 1) * 2, :], in_=xs[:]),
                   [t_xs], [t_xb])

        def tile_body(ti):
            t0 = ti * TS
            tsl = slice(t0, t0 + TS)
            load_x(ti)

            P.dma("sp", posi[R, :], posr[:, tsl], writes=[t_posi], key="pos")
            ang, t_ang = tmp.next()
            kf, t_kf = tmp.next()
            rr, t_rr = tmp.next()
            mm, t_mm = tmp.next()
            cosT, t_cos = W["rope"].next()
            sinT, t_sin = W["rope"].next()
            op("dve", I("tensor_copy", out=ang[R, :], in_=posi[R, :]), [t_posi], [t_ang])
            op("dve", I("tensor_scalar", out=ang[R, :], in0=ang[R, :], scalar1=vc[R, 4:5], scalar2=None, op0=ALU.mult), [t_ang, t_vc], [t_ang])
            op("dve", I("tensor_scalar", out=kf[R, :], in0=ang[R, :], scalar1=1.0 / TWO_PI, scalar2=None, op0=ALU.mult), [t_ang], [t_kf])
            op("dve", I("tensor_copy", out=posi[R, :], in_=kf[R, :]), [t_kf], [t_posi])
            op("dve", I("tensor_copy", out=kf[R, :], in_=posi[R, :]), [t_posi], [t_kf])
            op("dve", I("scalar_tensor_tensor", out=rr[R, :], in0=kf[R, :], scalar=-CW1, in1=ang[R, :], op0=ALU.mult, op1=ALU.add), [t_kf, t_ang], [t_rr])
            op("dve", I("scalar_tensor_tensor", out=rr[R, :], in0=kf[R, :], scalar=-CW2, in1=rr[R, :], op0=ALU.mult, op1=ALU.add), [t_kf, t_rr], [t_rr])
            op("dve", I("tensor_scalar", out=mm[R, :], in0=rr[R, :], scalar1=math.pi, scalar2=TWO_PI, op0=ALU.is_gt, op1=ALU.mult), [t_rr], [t_mm])
            op("dve", I("tensor_tensor", out=sinT[R, :], in0=rr[R, :], in1=mm[R, :], op=ALU.subtract), [t_rr, t_mm], [t_sin])
            op("dve", I("tensor_scalar", out=sinT[R, :], in0=sinT[R, :], scalar1=-math.pi, scalar2=math.pi, op0=ALU.max, op1=ALU.min), [t_sin], [t_sin])
            op("dve", I("tensor_scalar", out=rr[R, :], in0=rr[R, :], scalar1=math.pi / 2, scalar2=None, op0=ALU.add), [t_rr], [t_rr])
            op("dve", I("tensor_scalar", out=mm[R, :], in0=rr[R, :], scalar1=math.pi, scalar2=TWO_PI, op0=ALU.is_gt, op1=ALU.mult), [t_rr], [t_mm])
            op("dve", I("tensor_tensor", out=cosT[R, :], in0=rr[R, :], in1=mm[R, :], op=ALU.subtract), [t_rr, t_mm], [t_cos])
            op("dve", I("tensor_scalar", out=cosT[R, :], in0=cosT[R, :], scalar1=-math.pi, scalar2=math.pi, op0=ALU.max, op1=ALU.min), [t_cos], [t_cos])
            op("act", I("activation", out=sinT[R, :], in_=sinT[R, :], func=AF.Sin), [t_sin], [t_sin])
            op("act", I("activation", out=cosT[R, :], in_=cosT[R, :], func=AF.Sin), [t_cos], [t_cos])
            op("dve", I("tensor_scalar", out=sinT[R, :], in0=sinT[R, :], scalar1=vc[R, 5:6], scalar2=None, op0=ALU.mult), [t_sin, t_vc], [t_sin])

            if STOP == 1:
                return
            def fm_chunk(c):
                pc, t_pc = gen.next()
                for kc in range(8):
                    op("pe", I("matmul", pc[:], lhsT=wb[:, kc, c * 128:(c + 1) * 128], rhs=xbt[:, kc, :],
                                                               start=(kc == 0), stop=(kc == 7)), [t_wb, t_xb], [t_pc])
                return pc, t_pc

            vt, t_vt = vtm.next()
            for s4 in range(4):
                pv, t_pv = gen.next()
                for kc in range(8):
                    op("pe", I("matmul", pv[:, 0:256], lhsT=xbt[:, kc, s4 * 128:(s4 + 1) * 128],
                                                                      rhs=wb[:, kc, NFM * 128:NFM * 128 + 256], start=(kc == 0), stop=(kc == 7)),
                       [t_wb, t_xb], [t_pv])
                op("act", I("activation", out=vt[:, s4, :], in_=pv[:, 0:256], func=AF.Copy), [t_pv], [t_vt])

            if STOP == 11:
                return
            cqn = []
            sqs = []
            cq32s = []
            for c in range(2):
                pc, t_pc = fm_chunk(6 + c)
                cq32, t_cq32 = tmp.next()
                sq, t_sq = W["sq"].next()
                op("act", I("activation", out=cq32[:], in_=pc[:], func=AF.Copy), [t_pc], [t_cq32])
                op("act", I("activation", out=sq[:], in_=pc[:], func=AF.Square), [t_pc], [t_sq])
                sqs.append((sq, t_sq))
                cq32s.append((cq32, t_cq32))
            ms, t_ms = gen.next()
            for c in range(2):
                op("pe", I("matmul", ms[:], lhsT=ones_b[:], rhs=sqs[c][0][:], start=(c == 0), stop=(c == 1)),
                   [t_ones, sqs[c][1]], [t_ms])
            rs, t_rs = rstd_from(ms, t_ms, 256)
            for c in range(2):
                cn, t_cn = W["cn"].next()
                op("dve", I("scalar_tensor_tensor", out=cn[:], in0=cq32s[c][0][:], scalar=vc[:, 1 + c:2 + c], in1=rs[:],
                                                                             op0=ALU.mult, op1=ALU.mult), [cq32s[c][1], t_vc, t_rs], [t_cn])
                cqn.append((cn, t_cn))
            if STOP == 12:
                return
            pc, t_pc = fm_chunk(8)
            ck32, t_ck32 = tmp.next()
            sq, t_sq = W["sq"].next()
            op("act", I("activation", out=ck32[:], in_=pc[:], func=AF.Copy), [t_pc], [t_ck32])
            op("act", I("activation", out=sq[:], in_=pc[:], func=AF.Square), [t_pc], [t_sq])
            ms2, t_ms2 = gen.next()
            op("pe", I("matmul", ms2[:], lhsT=ones_b[:], rhs=sq[:], start=True, stop=True), [t_ones, t_sq], [t_ms2])
            rs2, t_rs2 = rstd_from(ms2, t_ms2, 128)
            ckn, t_ckn = W["cn"].next()
            op("dve", I("scalar_tensor_tensor", out=ckn[:], in0=ck32[:], scalar=vc[:, 3:4], in1=rs2[:], op0=ALU.mult, op1=ALU.mult),
               [t_ck32, t_vc, t_rs2], [t_ckn])

            if STOP == 13:
                return
            mg = []
            krs = []
            for c in range(2):
                pc, t_pc = fm_chunk(9 + c)
                mg.append(silu_from(pc, t_pc, 64))
                tr, t_tr = tmp.next()
                tab, t_tab = (cosT, t_cos) if c == 0 else (sinT, t_sin)
                op("dve", I("tensor_tensor", out=tr[R, :], in0=pc[R, :], in1=tab[R, :], op=ALU.mult),
                   [t_pc, t_tab], [t_tr])
                krs.append((tr, t_tr))
            for h in range(2):
                op("pool", I("tensor_tensor", out=KT[h][0][R, tsl], in0=krs[0][0][R, :], in1=krs[1][0][R, :], op=ALU.add),
                   [krs[0][1], krs[1][1]], [KT[h][1]])
            if STOP == 14:
                return
            VAR = os.environ.get("MK_VAR", "")
            qmb, t_qmb = W["qmb"].next()
            if "a" not in VAR:
                pc, t_pc = fm_chunk(11)
            if "b" not in VAR:
                op("act", I("activation", out=qmb[:], in_=pc[:], func=AF.Copy), [t_pc], [t_qmb])
            if "c" not in VAR:
                pc, t_pc = fm_chunk(12)
            if "d" not in VAR:
                sgm, t_sgm = silu_from(pc, t_pc, 128)

            if STOP == 2:
                return
            for h in range(2):
                pc, t_pc = fm_chunk(h)
                q32, t_q32 = W["q32"].next()
                op("act", I("activation", out=q32[:], in_=pc[:], func=AF.Copy), [t_pc], [t_q32])
                pc, t_pc = fm_chunk(2 + h)
                zf, t_zf = W["zf"].next()
                k32, t_k32 = W["k32"].next()
                op("act", I("activation", out=zf[:], in_=pc[:], func=AF.Exp, scale=-1.0), [t_pc], [t_zf])
                op("dve", I("tensor_scalar", out=zf[:], in0=zf[:], scalar1=1.0, scalar2=None, op0=ALU.add), [t_zf], [t_zf])
                op("dve", I("reciprocal", out=zf[:], in_=zf[:]), [t_zf], [t_zf])
                op("dve", I("tensor_scalar", out=zf[:], in0=zf[:], scalar1=oml[:, h:h + 1], scalar2=lb[:, h:h + 1],
                                                                 op0=ALU.mult, op1=ALU.add), [t_zf, t_oml, t_lb], [t_zf])
                op("pool", I("tensor_scalar", out=k32[:], in0=zf[:], scalar1=-1.0, scalar2=1.0, op0=ALU.mult, op1=ALU.add),
                   [t_zf], [t_k32])
                op("act", I("activation", out=zf[:], in_=zf[:], func=AF.Ln), [t_zf], [t_zf])
                pc, t_pc = fm_chunk(4 + h)
                sg, t_sg = silu_from(pc, t_pc, 128)

                if STOP == 21:
                    return
                bb, t_bb = W["bb"].next()
                eb, t_eb = W["eb"].next()
                qt, t_qt = W["qt"].next()
                kt, t_kt = W["kt"].next()
                kh, t_kh = W["kh"].next()
                op("dve", I("tensor_tensor_scan", out=bb[:], data0=scanm, data1=zf[:], initial=0.0, op0=ALU.mult, op1=ALU.add),
                   [t_cm32, t_zf], [t_bb])
                op("act", I("activation", out=eb[:], in_=bb[:], func=AF.Exp), [t_bb], [t_eb])
                enb, t_enb = zf, t_zf
                op("act", I("activation", out=enb[:], in_=bb[:], func=AF.Exp, scale=-1.0), [t_bb], [t_enb])
                op("pool", I("tensor_tensor", out=qt[:], in0=q32[:], in1=eb[:], op=ALU.mult), [t_q32, t_eb], [t_qt])
                op("pool", I("tensor_tensor", out=k32[:], in0=k32[:], in1=enb[:], op=ALU.mult), [t_k32, t_enb], [t_k32])
                op("pool", I("tensor_copy", out=kt[:], in_=k32[:]), [t_k32], [t_kt])
                for c8 in range(8):
                    op("pool", I("tensor_scalar", out=kh[:, c8 * 64:(c8 + 1) * 64], in0=k32[:, c8 * 64:(c8 + 1) * 64],
                        scalar1=eb[:, c8 * 64 + 63:c8 * 64 + 64], scalar2=None, op0=ALU.mult), [t_k32, t_eb], [t_kh])
                if STOP == 22:
                    return
                po_, t_po = pho, t_pho
                S32, t_S = St[h]
                for j in range(4):
                    js = slice(j * 128, (j + 1) * 128)
                    bx, t_bx = gen.next()
                    op("pe", I("matmul", bx[:, 0:128], lhsT=kt[:, js], rhs=qt[:, js], start=True, stop=True),
                       [t_kt, t_qt], [t_bx])
                    op("pe", I("matmul", bx[:, 128:256], lhsT=kh[:, js], rhs=ident[:], start=True, stop=True),
                       [t_kh, t_ident], [t_bx])
                    if STOP == 231:
                        return
                    am, t_am = am_r.next()
                    op("dve", I("tensor_tensor", out=am[:], in0=bx[:, 0:128], in1=mask2, op=ALU.mult), [t_bx, t_cm32], [t_am])
                    kmA, t_kmA = ktmA.next()
                    kmB, t_kmB = ktmB.next()
                    op("act", I("activation", out=kmA[0:64, :], in_=bx[0:64, 128:256], func=AF.Copy), [t_bx], [t_kmA])
                    op("act", I("activation", out=kmB[64:128, :], in_=bx[64:128, 128:256], func=AF.Copy), [t_bx], [t_kmB])
                    if STOP == 232:
                        return
                    op("pe", I("matmul",
                        po_[:, js], lhsT=vt[:, j, h * 128:(h + 1) * 128], rhs=am[:], start=True, stop=False), [t_vt, t_am], [t_po])
                    if STOP == 233:
                        return
                    by, t_by = gen.next()
                    for p2, (km, t_km) in enumerate(((kmA, t_kmA), (kmB, t_kmB))):
                        op("pe", I("matmul",
                            by[:, p2 * 128:(p2 + 1) * 128], lhsT=km[:], rhs=vt[:, j, h * 128:(h + 1) * 128], start=True, stop=True),
                           [t_km, t_vt], [t_by])
                        if STOP == 234:
                            return
                    for p2 in range(2):
                        cs = slice(j * 128 + p2 * 64, j * 128 + p2 * 64 + 64)
                        sbc, t_sbc = Sb[h][sb_i[h] % 2]
                        sbn, t_sbn = Sb[h][(sb_i[h] + 1) % 2]
                        sb_i[h] += 1
                        op("pe", I("matmul", po_[:, cs], lhsT=sbc[:], rhs=qt[:, cs], start=False, stop=(p2 == 1)), [t_sbc, t_qt], [t_po])
                        if STOP == 235:
                            return
                        ecol = j * 128 + p2 * 64 + 63
                        op("dve", I("scalar_tensor_tensor", out=S32[:], in0=S32[:], scalar=eb[:, ecol:ecol + 1], in1=by[:, p2 * 128:(p2 + 1) * 128], op0=ALU.mult, op1=ALU.add),
                           [t_S, t_eb, t_by], [t_S])
                        if STOP == 236:
                            return
                        op("pool", I("tensor_copy", out=sbn[:], in_=S32[:]), [t_S], [t_sbn])
                if STOP == 23:
                    return
                o32, t_o32 = tmp.next()
                sq, t_sq = W["sq"].next()
                op("act", I("activation", out=o32[:], in_=po_[:], func=AF.Copy), [t_po], [t_o32])
                op("act", I("activation", out=sq[:], in_=po_[:], func=AF.Square), [t_po], [t_sq])
                ms3, t_ms3 = gen.next()
                op("pe", I("matmul", ms3[:], lhsT=ones_b[:], rhs=sq[:], start=True, stop=True), [t_ones, t_sq], [t_ms3])
                rs3, t_rs3 = rstd_from(ms3, t_ms3, 128)
                yo, t_yo = W["yo"].next()
                op("dve", I("scalar_tensor_tensor", out=o32[:], in0=o32[:], scalar=vc[:, 0:1], in1=rs3[:], op0=ALU.mult, op1=ALU.mult),
                   [t_o32, t_vc, t_rs3], [t_o32])
                op("pool", I("tensor_tensor", out=yo[:], in0=o32[:], in1=sg[:], op=ALU.mult), [t_o32, t_sg], [t_yo])
                ty_ = Tk("y")
                P.dma("sp", yT[h * 128:(h + 1) * 128, tsl], yo[:], reads=[t_yo], writes=[ty_], key=f"y{W['yo'].i % 2}")
                y_tks.append(ty_)

            if STOP == 3:
                return
            for h in range(2):
                pk_, t_pk_ = gen.next()
                op("pe", I("matmul", pk_[:], lhsT=wukvb[:, h * 128:(h + 1) * 128], rhs=ckn[:], start=True, stop=True),
                   [t_wukvb, t_ckn], [t_pk_])
                op("act", I("activation", out=KT[h][0][0:64, tsl], in_=pk_[0:64, :], func=AF.Copy), [t_pk_], [KT[h][1]])
            pv, t_pv = gen.next()
            for s4 in range(4):
                op("pe", I("matmul", pv[:, s4 * 128:(s4 + 1) * 128], lhsT=ckn[:, s4 * 128:(s4 + 1) * 128],
                                                                    rhs=wukvb[:, 256:384], start=True, stop=True),
                   [t_wukvb, t_ckn], [t_pv])
            op("act", I("activation", out=VV[:, ti * 4:ti * 4 + 4, :, 0:64],
                                                     in_=pv[:].rearrange("p (s h d) -> p s h d", s=4, h=2), func=AF.Copy),
               [t_pv], [t_VV])
            QTs = []
            for h in range(2):
                pA, t_pA = gen.next()
                pB, t_pB = gen.next()
                for ver, (pp, t_pp) in enumerate(((pA, t_pA), (pB, t_pB))):
                    for c in range(2):
                        col = ver * 192 + h * 96
                        op("pe", I("matmul", pp[0:96, :], lhsT=wuqb[:, c, col:col + 96], rhs=cqn[c][0][:],
                                                                                  start=(c == 0), stop=(c == 1)), [t_wuqb, cqn[c][1]], [t_pp])
                QT, t_QT = W["QT"].next()
                op("act", I("activation", out=QT[0:64, :], in_=pA[0:64, :], func=AF.Copy), [t_pA], [t_QT])
                t1, t_t1 = tmp.next()
                t2, t_t2 = tmp.next()
                op("dve", I("tensor_tensor", out=t1[R, :], in0=pA[R, :], in1=cosT[R, :], op=ALU.mult), [t_pA, t_cos], [t_t1])
                op("dve", I("tensor_tensor", out=t2[R, :], in0=pB[R, :], in1=sinT[R, :], op=ALU.mult), [t_pB, t_sin], [t_t2])
                op("pool", I("tensor_tensor", out=QT[R, :], in0=t1[R, :], in1=t2[R, :], op=ALU.add), [t_t1, t_t2], [t_QT])
                QTs.append((QT, t_QT))
            if STOP == 4:
                return
            nk = 4 * (ti + 1)
            for h in range(2):
                QT, t_QT = QTs[h]
                pacc, t_pacc = po[h]
                for kt_i in range(nk):
                    ks = slice(kt_i * 128, (kt_i + 1) * 128)
                    sc, t_sc = psc.next()
                    op("pe", I("matmul", sc[:], lhsT=KT[h][0][0:96, ks], rhs=QT[0:96, :], start=True, stop=True),
                       [KT[h][1], t_QT], [t_sc])
                    pt, t_pt2 = W["pt"].next()
                    op("act", I("activation", out=pt[:], in_=sc[:], func=AF.Exp, scale=SC_MLA), [t_sc], [t_pt2])
                    jd = kt_i - 4 * ti
                    if jd >= 0:
                        off = 384 - 128 * jd
                        op("pool", I("tensor_tensor", out=pt[:], in0=pt[:], in1=caus[:, off:off + 512], op=ALU.mult),
                           [t_pt2, t_caus], [t_pt2])
                    op("pe", I("matmul", pacc[0:65, :], lhsT=VV[:, kt_i, h, :], rhs=pt[:],
                                                                                          start=(kt_i == 0), stop=(kt_i == nk - 1)), [t_VV, t_pt2], [t_pacc])
                rsum, t_rsum = tmp.next()
                op("dve", I("memset", rsum[:], 0.0), [], [t_rsum])
                op("dve", I("reciprocal", out=rsum[64:65, :], in_=pacc[64:65, :]), [t_pacc, t_rsum], [t_rsum])
                pb, t_pb = gen.next()
                op("pe", I("matmul", pb[:], lhsT=ones_f[:], rhs=rsum[:], start=True, stop=True),
                   [t_onesf, t_rsum], [t_pb])
                rb, t_rb = tmp.next()
                op("act", I("activation", out=rb[0:64, :], in_=pb[0:64, :], func=AF.Copy), [t_pb], [t_rb])
                yo, t_yo = W["yo"].next()
                sg, t_sg = mg[h]
                op("dve", I("tensor_tensor", out=rb[0:64, :], in0=pacc[0:64, :], in1=rb[0:64, :], op=ALU.mult), [t_pacc, t_rb], [t_rb])
                op("pool", I("tensor_tensor", out=yo[0:64, :], in0=rb[0:64, :], in1=sg[0:64, :], op=ALU.mult), [t_rb, t_sg], [t_yo])
                ty_ = Tk("y")
                P.dma("sp", yT[256 + h * 64:256 + (h + 1) * 64, tsl], yo[0:64, :], reads=[t_yo], writes=[ty_], key=f"y{W['yo'].i % 2}")
                y_tks.append(ty_)

            if STOP == 5:
                return
            pom, t_pom = gen.next()
            psm, t_psm = gen.next()
            for mt in range(2):
                sc, t_sc = psc.next()
                op("pe", I("matmul", sc[:], lhsT=kmT[:, mt * 128:(mt + 1) * 128], rhs=qmb[:], start=True, stop=True),
                   [t_kmT, t_qmb], [t_sc])
                pt, t_pt2 = W["pt"].next()
                op("act", I("activation", out=pt[:], in_=sc[:], func=AF.Exp, scale=SC_MEM), [t_sc], [t_pt2])
                op("pe", I("matmul", pom[:], lhsT=vm[:, mt, :], rhs=pt[:], start=(mt == 0), stop=(mt == 1)), [t_vm, t_pt2], [t_pom])
                op("pe", I("matmul", psm[:], lhsT=ones_b[:], rhs=pt[:], start=(mt == 0), stop=(mt == 1)), [t_ones, t_pt2], [t_psm])
            rm, t_rm = tmp.next()
            op("dve", I("reciprocal", out=rm[:], in_=psm[:]), [t_psm], [t_rm])
            op("dve", I("tensor_tensor", out=rm[:], in0=pom[:], in1=rm[:], op=ALU.mult), [t_pom, t_rm], [t_rm])
            yo, t_yo = W["yo"].next()
            op("pool", I("tensor_tensor", out=yo[:], in0=rm[:], in1=sgm[:], op=ALU.mult), [t_rm, t_sgm], [t_yo])
            ty_ = Tk("y")
            P.dma("sp", yT[384:512, tsl], yo[:], reads=[t_yo], writes=[ty_], key=f"y{W['yo'].i % 2}")
            y_tks.append(ty_)

        for ti in range(nt):
            tile_body(ti)
        P.wait_all("sp", y_tks)
        P.emit()
    return nc


HG_W = 1024
OFF = dict(hq=0, hf=1024, hi=2048, hgate=3072, cq=4096, ckv=4352, kr=4480, mgate=4512, mq=5024, mgt=5536)


def consts_mask():
    m = np.zeros((128, 512 + 128 + 896), np.float32)
    f = np.arange(512)
    m[:, 0:512] = (f % 64 != 0).astype(np.float32)[None, :]
    s = np.arange(128)[:, None]
    t = np.arange(128)[None, :]
    m[:, 512:640] = ((s <= t) & (s // 64 == t // 64)).astype(np.float32)
    g = np.arange(896)[None, :]
    m[:, 640:] = (g - 384 >= s).astype(np.float32)
    return m


def mixer_inputs(l, xT_b, inp, b, g):
    w = inp["w_in"][l]
    swap = np.concatenate([np.arange(16, 32), np.arange(0, 16)])
    z64 = np.zeros((D, 32), np.float32)
    cols = []
    for h in range(2):
        cols.append(w[:, OFF["hq"] + (2 * g + h) * 128:OFF["hq"] + (2 * g + h + 1) * 128])
    for h in range(2):
        cols.append(w[:, OFF["hf"] + (2 * g + h) * 128:OFF["hf"] + (2 * g + h + 1) * 128])
    for h in range(2):
        cols.append(w[:, OFF["hgate"] + (2 * g + h) * 128:OFF["hgate"] + (2 * g + h + 1) * 128])
    cols.append(w[:, OFF["cq"]:OFF["cq"] + 256])
    cols.append(w[:, OFF["ckv"]:OFF["ckv"] + 128])
    kr = w[:, OFF["kr"]:OFF["kr"] + 32]
    for h in range(2):
        cols.append(w[:, OFF["mgate"] + (2 * g + h) * 64:OFF["mgate"] + (2 * g + h + 1) * 64])
        cols.append(kr if h == 0 else kr[:, swap])
        cols.append(z64)
    cols.append(w[:, OFF["mq"] + g * 128:OFF["mq"] + (g + 1) * 128])
    cols.append(w[:, OFF["mgt"] + g * 128:OFF["mgt"] + (g + 1) * 128])
    cols.append(w[:, OFF["hi"] + 2 * g * 128:OFF["hi"] + (2 * g + 2) * 128])
    win = np.ascontiguousarray(np.concatenate(cols, axis=1))
    assert win.shape == (D, NCOL)
    lbl = np.zeros((128, 4), np.float32)
    for ll in range(2):
        for h in range(2):
            lbl[:, ll * 2 + h] = inp["hgrn_lb_logits"][ll, (2 * g + h) * 128:(2 * g + h + 1) * 128]
    vecs = np.zeros((128, 8), np.float32)
    vecs[:, 0] = inp["hgrn_out_norm"][l]
    vecs[:, 1] = inp["mla_q_norm"][l, 0:128]
    vecs[:, 2] = inp["mla_q_norm"][l, 128:256]
    vecs[:, 3] = inp["mla_kv_norm"][l]
    invf = (1.0 / (10000.0 ** (np.arange(0, 32, 2, dtype=np.float32) / np.float32(32)))).astype(np.float32)
    vecs[64:96, 4] = np.concatenate([invf, invf])
    vecs[64:80, 5] = -1.0
    vecs[80:96, 5] = 1.0
    uq = inp["w_mla_uq"][l]
    parts = []
    for ver in range(2):
        for h in range(2):
            hh = 2 * g + h
            blk = uq[:, hh * 96:(hh + 1) * 96]
            if ver == 1:
                blk = np.concatenate([blk[:, 0:64], blk[:, 64:96][:, swap]], axis=1)
            parts.append(blk)
    wuq = np.ascontiguousarray(np.concatenate(parts, axis=1))
    uk = inp["w_mla_uk"][l]
    uv = inp["w_mla_uv"][l]
    uk0 = uk[:, (2 * g) * 64:(2 * g + 1) * 64]
    uk1 = uk[:, (2 * g + 1) * 64:(2 * g + 2) * 64]
    wukv = np.ascontiguousarray(np.concatenate([uk0, uk1, uk1, uk0, uv[:, 2 * g * 64:(2 * g + 2) * 64]], axis=1))
    memT = np.ascontiguousarray(inp["mem"][b].T)
    wmkv = np.ascontiguousarray(np.concatenate([inp["w_mem_k"][l][:, g * 128:(g + 1) * 128], inp["w_mem_v"][l][:, g * 128:(g + 1) * 128]], axis=1))
    posr = np.ascontiguousarray(np.broadcast_to(inp["positions"][b][None, :], (32, S))).astype(np.int32)
    return dict(xT=xT_b, win=win, lbl=lbl, vecs=vecs, wuq=wuq, wukv=wukv, memT=memT, wmkv=wmkv, posr=posr, cmask=consts_mask())


def run_mixer(l, x, inp, nt=S // TS):
    nc = build_mixer(l, nt)
    in_maps = []
    for b in range(2):
        xT_b = np.ascontiguousarray(x[b].T)
        for g in range(4):
            in_maps.append(mixer_inputs(l, xT_b, inp, b, g))
    res = run_bass_kernel_spmd(nc, in_maps, core_ids=list(range(8)))
    y = np.zeros((2, S, 2048), np.float32)
    for b in range(2):
        for g in range(4):
            yT = res.results[b * 4 + g]["yT"]
            for h in range(2):
                y[b, :, (2 * g + h) * 128:(2 * g + h + 1) * 128] = yT[h * 128:(h + 1) * 128].T
                y[b, :, 1024 + (2 * g + h) * 64:1024 + (2 * g + h + 1) * 64] = yT[256 + h * 64:256 + (h + 1) * 64].T
            y[b, :, 1536 + g * 128:1536 + (g + 1) * 128] = yT[384:512].T
    return y


NTOK = 2048
AX = mybir.AxisListType.X


def build_outproj():
    nc = bass.Bass("TRN2", target_bir_lowering=False)

    def dr(n, s, dt=F32, kind="ExternalInput"):
        return nc.dram_tensor(n, list(s), dt, kind=kind).ap()

    yT = dr("yT", [2048, NTOK])
    xres = dr("xres", [NTOK, D])
    wout = dr("wout", [2048, D])
    lng = dr("lng", [128, D])
    lnb = dr("lnb", [128, D])
    xo = dr("xo", [NTOK, D], kind="ExternalOutput")

    with ExitStack() as es:
        P = Prog(nc, es)
        op = P.op

        def sbt(name, shape, dt=F32):
            return P.sb(name, shape, dt), Tk(name)

        wob, t_wob = sbt("wob", [128, 16, D], BF16)
        g_t, t_g = sbt("g_t", [128, D])
        b_t, t_b = sbt("b_t", [128, D])
        eps_t, t_eps = sbt("eps_t", [128, 1])
        op("pool", I("memset", eps_t[:], LN_EPS), [], [t_eps])
        P.dma("sp", g_t[:], lng, writes=[t_g], key="g")
        P.dma("sp", b_t[:], lnb, writes=[t_b], key="b")
        wst = Ring(P, "wst", 2, [128, D], F32)
        for kc in range(16):
            st, t_st = wst.next()
            P.dma("sp", st[:], wout[kc * 128:(kc + 1) * 128, :], writes=[t_st], key=f"w{kc % 2}")
            op("dve" if kc % 2 else "pool", I("tensor_copy", out=wob[:, kc, :], in_=st[:]), [t_st], [t_wob])

        y32 = Ring(P, "y32", 2, [128, 16, 128], F32)
        yb = Ring(P, "yb", 2, [128, 16, 128], BF16)
        xr = Ring(P, "xr", 2, [128, D], F32)
        z = Ring(P, "z", 2, [128, D], F32)
        sq = Ring(P, "sq", 1, [128, D], F32)
        st4 = Ring(P, "st4", 2, [128, 4], F32)
        pacc = Ring(P, "pacc", 4, [128, 512], F32, psum=True)
        o_tks = []
        for ti in range(NTOK // 128):
            ts_ = slice(ti * 128, (ti + 1) * 128)
            ys, t_ys = y32.next()
            ybt, t_yb = yb.next()
            P.dma("sp", ys[:], yT[:, ts_].rearrange("(c p) t -> p c t", p=128), writes=[t_ys], key=f"y{ti % 2}")
            op("pool", I("tensor_copy", out=ybt[:], in_=ys[:]), [t_ys], [t_yb])
            xs, t_xs = xr.next()
            P.dma("sp", xs[:], xres[ts_, :], writes=[t_xs], key=f"x{ti % 2}")
            zt, t_z = z.next()
            for hf in range(2):
                pa, t_pa = pacc.next()
                for kc in range(16):
                    op("pe", I("matmul", pa[:], lhsT=ybt[:, kc, :], rhs=wob[:, kc, hf * 512:(hf + 1) * 512],
                               start=(kc == 0), stop=(kc == 15)), [t_yb, t_wob], [t_pa])
                op("dve", I("scalar_tensor_tensor", out=zt[:, hf * 512:(hf + 1) * 512], in0=xs[:, hf * 512:(hf + 1) * 512],
                            scalar=ALPHA, in1=pa[:], op0=ALU.mult, op1=ALU.add), [t_xs, t_pa], [t_z])
            s4, t_s4 = st4.next()
            op("dve", I("reduce_sum", out=s4[:, 0:1], in_=zt[:], axis=AX), [t_z], [t_s4])
            op("dve", I("tensor_scalar", out=s4[:, 0:1], in0=s4[:, 0:1], scalar1=-1.0 / D, scalar2=None, op0=ALU.mult), [t_s4], [t_s4])
            op("dve", I("tensor_scalar", out=zt[:], in0=zt[:], scalar1=s4[:, 0:1], scalar2=None, op0=ALU.add), [t_z, t_s4], [t_z])
            sqt, t_sq = sq.next()
            op("act", I("activation", out=sqt[:], in_=zt[:], func=AF.Square), [t_z], [t_sq])
            op("dve", I("reduce_sum", out=s4[:, 1:2], in_=sqt[:], axis=AX), [t_sq, t_s4], [t_s4])
            op("act", I("activation", out=s4[:, 2:3], in_=s4[:, 1:2], func=AF.Ln, bias=eps_t[:], scale=1.0 / D), [t_s4, t_eps], [t_s4])
            op("act", I("activation", out=s4[:, 3:4], in_=s4[:, 2:3], func=AF.Exp, scale=-0.5), [t_s4], [t_s4])
            op("dve", I("scalar_tensor_tensor", out=zt[:], in0=zt[:], scalar=s4[:, 3:4], in1=g_t[:], op0=ALU.mult, op1=ALU.mult),
               [t_z, t_s4, t_g], [t_z])
            op("pool", I("tensor_tensor", out=zt[:], in0=zt[:], in1=b_t[:], op=ALU.add), [t_z, t_b], [t_z])
            to_ = Tk("o")
            P.dma("sp", xo[ts_, :], zt[:], reads=[t_z], writes=[to_], key=f"o{ti % 2}")
            o_tks.append(to_)
        P.wait_all("sp", o_tks)
        P.emit()
    return nc


def run_outproj(l, x, y, inp):
    nc = build_outproj()
    xf = x.reshape(2 * S, D)
    yf = y.reshape(2 * S, 2048)
    g = np.ascontiguousarray(np.broadcast_to(inp["ln_g"][l][None, :], (128, D))).astype(np.float32)
    b = np.ascontiguousarray(np.broadcast_to(inp["ln_b"][l][None, :], (128, D))).astype(np.float32)
    wo = np.ascontiguousarray(inp["w_out"][l])
    in_maps = []
    for c in range(8):
        sl = slice(c * NTOK, (c + 1) * NTOK)
        in_maps.append(dict(yT=np.ascontiguousarray(yf[sl].T), xres=np.ascontiguousarray(xf[sl]), wout=wo, lng=g, lnb=b))
    res = run_bass_kernel_spmd(nc, in_maps, core_ids=list(range(8)))
    out = np.concatenate([res.results[c]["xo"] for c in range(8)], axis=0)
    return out.reshape(2, S, D)


def kernel(**inp):
    inp = {k: np.asarray(v) for k, v in inp.items()}
    x = inp["x"].astype(np.float32)
    for l in range(DEPTH):
        y = run_mixer(l, x, inp)
        x = run_outproj(l, x, y, inp)
    return x.astype(np.float32)
```
